# Optimizing a Trainium2 kernel written in Bass

```python
import math
import jax, jax.numpy as jnp
from jax import lax
import numpy as np

D_MODEL = 2048
BATCH = 4
SEQ = 4096
DEPTH = 4
DEC_BATCH = 1
DEC_SEQ = 8192
PAST_LEN = 128

HEAD_DIM = 64
A_HEADS = 8
A_KV = 2
A_HALF_WIN = 128
B_HEADS = 8
B_PATTERNS = ((128, 1), (512, 4), (2048, 16))
C_HEADS = 4
C_VDIM = 2 * HEAD_DIM
D_HEADS = 8
D_KV = 2
GRID_W = 64
ROPE_THETA = 10000.0
ROPE_PAIRS = HEAD_DIM // 4
QBLK = 128
NUM_BUCKETS = 32
RELPOS_MAX_DIST = 1024
N_BIAS_HEADS = A_HEADS + B_HEADS + C_HEADS
FF_DIM = -(-8 * D_MODEL // (3 * 256)) * 256
N_MOD = 6

SPLIT_SIZES = (
    A_HEADS * HEAD_DIM, A_KV * HEAD_DIM, A_KV * HEAD_DIM,
    B_HEADS * HEAD_DIM, B_HEADS * HEAD_DIM, B_HEADS * HEAD_DIM,
    C_HEADS * 2 * HEAD_DIM, C_HEADS * 2 * HEAD_DIM, C_HEADS * C_VDIM,
    D_HEADS * HEAD_DIM, D_KV * HEAD_DIM, D_KV * HEAD_DIM,
)
IN_WIDTH = sum(SPLIT_SIZES)
MIX_WIDTH = A_HEADS * HEAD_DIM + B_HEADS * HEAD_DIM + C_HEADS * C_VDIM + D_HEADS * HEAD_DIM
NEG_INF = -1e30

kernel_name = "hybrid_parallel_head_group_encoder"


def rms_norm(x, g, eps=1e-6):
    xf = x.astype(jnp.float32)
    y = xf * lax.rsqrt(jnp.mean(xf * xf, axis=-1, keepdims=True) + eps)
    return (y * g.astype(jnp.float32)).astype(x.dtype)


def relpos_bucket(rel):
    half = NUM_BUCKETS // 2
    max_exact = half // 2
    n = jnp.abs(rel)
    nf = jnp.maximum(n, 1).astype(jnp.float32)
    large = max_exact + (jnp.log(nf / max_exact) / math.log(RELPOS_MAX_DIST / max_exact)
                         * (half - max_exact)).astype(jnp.int32)
    large = jnp.minimum(large, half - 1)
    return jnp.where(rel > 0, half, 0) + jnp.where(n < max_exact, n, large)


def relpos_bias(table, rel):
    return jnp.moveaxis(table[relpos_bucket(rel)].astype(jnp.float32), -1, 0)


def banded_attention(q, k, v, half_win, stride, bias_tab, sink=None):
    N, L, KV, G, d = q.shape
    blk = half_win
    nb = -(-L // blk)
    pad = nb * blk - L
    qb = jnp.pad(q, ((0, 0), (0, pad), (0, 0), (0, 0), (0, 0))).reshape(N, nb, blk, KV, G, d)

    def windows(t):
        tp = jnp.pad(t, ((0, 0), (blk, blk + pad), (0, 0), (0, 0))).reshape(N, nb + 2, blk, KV, t.shape[-1])
        return jnp.concatenate([tp[:, :-2], tp[:, 1:-1], tp[:, 2:]], axis=2)

    kw, vw = windows(k), windows(v)
    rel = jnp.arange(3 * blk)[None, :] - blk - jnp.arange(blk)[:, None]
    kpos = (jnp.arange(nb) * blk)[:, None, None] + jnp.arange(blk)[None, :, None] + rel[None]
    valid = (jnp.abs(rel) <= half_win)[None] & (kpos >= 0) & (kpos < L)
    bias = relpos_bias(bias_tab, rel * stride).reshape(KV, G, blk, 3 * blk)
    s = jnp.einsum('nbqhgd,nbkhd->nbhgqk', qb, kw, preferred_element_type=jnp.float32) * (d ** -0.5)
    s = jnp.where(valid[None, :, None, None], s + bias, NEG_INF)
    m = jnp.max(s, axis=-1, keepdims=True)
    if sink is not None:
        sk = sink.astype(jnp.float32).reshape(1, 1, KV, G, 1, 1)
        m = jnp.maximum(m, sk)
    p = jnp.exp(s - m)
    denom = jnp.sum(p, axis=-1, keepdims=True)
    if sink is not None:
        denom = denom + jnp.exp(sk - m)
    o = jnp.einsum('nbhgqk,nbkhe->nbqhge', (p / denom).astype(v.dtype), vw)
    lse = jnp.moveaxis((m + jnp.log(denom))[..., 0], -1, 2)
    o = o.reshape(N, nb * blk, KV, G, -1)[:, :L]
    lse = lse.reshape(N, nb * blk, KV, G)[:, :L]
    return o, lse


def to_strided(t, r):
    B, T = t.shape[:2]
    t = t.reshape(B, T // r, r, *t.shape[2:])
    return jnp.swapaxes(t, 1, 2).reshape(B * r, T // r, *t.shape[3:])


def from_strided(t, B):
    N, L = t.shape[:2]
    r = N // B
    t = t.reshape(B, r, L, *t.shape[2:])
    return jnp.swapaxes(t, 1, 2).reshape(B, L * r, *t.shape[3:])


def dilated_mixture(q, k, v, bias_tab):
    B = q.shape[0]
    outs, lses = [], []
    for window, r in B_PATTERNS:
        o, lse = banded_attention(to_strided(q, r)[:, :, :, None], to_strided(k, r), to_strided(v, r),
                                  window // (2 * r), r, bias_tab)
        outs.append(from_strided(o[:, :, :, 0], B))
        lses.append(from_strided(lse[..., 0], B))
    w = jax.nn.softmax(jnp.stack(lses), axis=0)
    out = jnp.einsum('pbth,pbthd->bthd', w, jnp.stack(outs).astype(jnp.float32))
    return out.astype(q.dtype)


def diff_attention(q, k, v, lam, bias_tab, subln_g, lam_init):
    B, T, H, _, d = q.shape
    nb = T // QBLK
    qb = jnp.moveaxis(q.reshape(B, nb, QBLK, H, 2, d), 1, 0)
    kpos = jnp.arange(T)

    def block(args):
        qblk, i = args
        rel = kpos[None, :] - (i * QBLK + jnp.arange(QBLK))[:, None]
        bias = relpos_bias(bias_tab, rel)
        s = jnp.einsum('bqhmd,bkhmd->bhmqk', qblk, k, preferred_element_type=jnp.float32) * (d ** -0.5)
        p = jax.nn.softmax(s + bias[None, :, None], axis=-1)
        a = p[:, :, 0] - lam * p[:, :, 1]
        return jnp.einsum('bhqk,bkhe->bqhe', a.astype(v.dtype), v)

    o = lax.map(block, (qb, jnp.arange(nb)))
    o = jnp.moveaxis(o, 0, 1).reshape(B, T, H, -1)
    return rms_norm(o, subln_g) * (1.0 - lam_init)


def dense_gqa(q, k, v):
    B, T, KV, G, d = q.shape
    nb = T // QBLK
    qb = jnp.moveaxis(q.reshape(B, nb, QBLK, KV, G, d), 1, 0)

    def block(qblk):
        s = jnp.einsum('bqhgd,bkhd->bhgqk', qblk, k, preferred_element_type=jnp.float32) * (d ** -0.5)
        p = jax.nn.softmax(s, axis=-1)
        return jnp.einsum('bhgqk,bkhd->bqhgd', p.astype(v.dtype), v)

    o = lax.map(block, qb)
    return jnp.moveaxis(o, 0, 1).reshape(B, T, KV * G * d)


def axial_rope_tables(T):
    rows = T // GRID_W
    r_idx, c_idx = jnp.meshgrid(jnp.arange(rows), jnp.arange(GRID_W), indexing='ij')
    pos = jnp.stack([r_idx.reshape(-1), c_idx.reshape(-1)], axis=-1).astype(jnp.float32)
    inv_freq = ROPE_THETA ** (-jnp.arange(ROPE_PAIRS, dtype=jnp.float32) / ROPE_PAIRS)
    ang = pos[:, :, None] * inv_freq
    return jnp.cos(ang), jnp.sin(ang)


def apply_axial_rope(x, cos, sin):
    xf = x.astype(jnp.float32).reshape(*x.shape[:-1], 2, 2, ROPE_PAIRS)
    x1, x2 = xf[..., 0, :], xf[..., 1, :]
    c, s = cos[:, None], sin[:, None]
    out = jnp.stack([x1 * c - x2 * s, x2 * c + x1 * s], axis=-2)
    return out.reshape(x.shape).astype(x.dtype)


def encoder_trunk(x, c, w_mod, b_mod, norm_mix, norm_ffn, w_in, w_out, qk_gain, sink_a,
                  relpos_table, diff_lambda, diff_subln, w_gate_up, w_down):
    B, T, _ = x.shape
    cos, sin = axial_rope_tables(T)
    tab_a = relpos_table[:, :A_HEADS]
    tab_b = relpos_table[:, A_HEADS:A_HEADS + B_HEADS]
    tab_c = relpos_table[:, A_HEADS + B_HEADS:]
    split_at = [int(i) for i in np.cumsum(SPLIT_SIZES)[:-1]]
    for l in range(DEPTH):
        mod = (jax.nn.silu(c) @ w_mod[l] + b_mod[l]).reshape(B, N_MOD, D_MODEL)[:, :, None, :]
        shift_m, scale_m, gate_m, shift_f, scale_f, gate_f = [mod[:, j] for j in range(N_MOD)]
        h = rms_norm(x, norm_mix[l]) * (1 + scale_m) + shift_m
        (qa, ka, va, qb, kb, vb, qc, kc, vc, qd, kd, vd) = jnp.split(h @ w_in[l], split_at, axis=-1)

        qa = rms_norm(qa.reshape(B, T, A_HEADS, HEAD_DIM), qk_gain[l, 0, 0]).reshape(
            B, T, A_KV, A_HEADS // A_KV, HEAD_DIM)
        ka = rms_norm(ka.reshape(B, T, A_KV, HEAD_DIM), qk_gain[l, 0, 1])
        out_a, _ = banded_attention(qa, ka, va.reshape(B, T, A_KV, HEAD_DIM), A_HALF_WIN, 1, tab_a,
                                    sink_a[l].reshape(A_KV, A_HEADS // A_KV))
        out_a = out_a.reshape(B, T, A_HEADS * HEAD_DIM)

        qb = rms_norm(qb.reshape(B, T, B_HEADS, HEAD_DIM), qk_gain[l, 1, 0])
        kb = rms_norm(kb.reshape(B, T, B_HEADS, HEAD_DIM), qk_gain[l, 1, 1])
        out_b = dilated_mixture(qb, kb, vb.reshape(B, T, B_HEADS, HEAD_DIM), tab_b).reshape(
            B, T, B_HEADS * HEAD_DIM)

        lam_init = 0.8 - 0.6 * math.exp(-0.3 * l)
        lv = diff_lambda[l].astype(jnp.float32)
        lam = jnp.exp(jnp.sum(lv[0] * lv[1])) - jnp.exp(jnp.sum(lv[2] * lv[3])) + lam_init
        qc = rms_norm(qc.reshape(B, T, C_HEADS, 2, HEAD_DIM), qk_gain[l, 2, 0])
        kc = rms_norm(kc.reshape(B, T, C_HEADS, 2, HEAD_DIM), qk_gain[l, 2, 1])
        out_c = diff_attention(qc, kc, vc.reshape(B, T, C_HEADS, C_VDIM), lam, tab_c, diff_subln[l],
                               lam_init).reshape(B, T, C_HEADS * C_VDIM)

        qd = apply_axial_rope(rms_norm(qd.reshape(B, T, D_HEADS, HEAD_DIM), qk_gain[l, 3, 0]), cos, sin)
        kd = apply_axial_rope(rms_norm(kd.reshape(B, T, D_KV, HEAD_DIM), qk_gain[l, 3, 1]), cos, sin)
        out_d = dense_gqa(qd.reshape(B, T, D_KV, D_HEADS // D_KV, HEAD_DIM), kd,
                          vd.reshape(B, T, D_KV, HEAD_DIM))

        mix = jnp.concatenate([out_a, out_b, out_c, out_d], axis=-1) @ w_out[l]
        x = x + gate_m * mix

        h = rms_norm(x, norm_ffn[l]) * (1 + scale_f) + shift_f
        g, u = jnp.split(h @ w_gate_up[l], 2, axis=-1)
        x = x + gate_f * ((jax.nn.silu(g) * u) @ w_down[l])
    return x


def setup_inputs(seed: int = 0) -> dict:
    key = jax.random.key(seed)
    ks = jax.random.split(key, 18)

    def nrm(k, shape, s):
        return jax.random.normal(k, shape, jnp.float32) * s

    b_mod = nrm(ks[5], (DEPTH, N_MOD, D_MODEL), 0.02)
    b_mod = b_mod.at[:, 2].add(1.0).at[:, 5].add(1.0).reshape(DEPTH, N_MOD * D_MODEL)
    return {
        "x_prompt": nrm(ks[0], (BATCH, SEQ, D_MODEL), 1.0),
        "x_sample": nrm(ks[1], (DEC_BATCH, DEC_SEQ, D_MODEL), 1.0),
        "c_prompt": nrm(ks[2], (BATCH, D_MODEL), 1.0),
        "c_sample": nrm(ks[3], (DEC_BATCH, D_MODEL), 1.0),
        "w_mod": nrm(ks[4], (DEPTH, D_MODEL, N_MOD * D_MODEL), 0.2 * D_MODEL ** -0.5),
        "b_mod": b_mod,
        "norm_mix": 1.0 + nrm(ks[6], (DEPTH, D_MODEL), 0.02),
        "norm_ffn": 1.0 + nrm(ks[7], (DEPTH, D_MODEL), 0.02),
        "w_in": nrm(ks[8], (DEPTH, D_MODEL, IN_WIDTH), D_MODEL ** -0.5),
        "w_out": nrm(ks[9], (DEPTH, MIX_WIDTH, D_MODEL), MIX_WIDTH ** -0.5),
        "qk_gain": 1.0 + nrm(ks[10], (DEPTH, 4, 2, HEAD_DIM), 0.02),
        "sink_a": nrm(ks[11], (DEPTH, A_HEADS), 0.5),
        "relpos_table": nrm(ks[12], (NUM_BUCKETS, N_BIAS_HEADS), 0.5),
        "diff_lambda": nrm(ks[13], (DEPTH, 4, HEAD_DIM), 0.1),
        "diff_subln": 1.0 + nrm(ks[14], (DEPTH, C_VDIM), 0.02),
        "w_gate_up": nrm(ks[15], (DEPTH, D_MODEL, 2 * FF_DIM), D_MODEL ** -0.5),
        "w_down": nrm(ks[16], (DEPTH, FF_DIM, D_MODEL), FF_DIM ** -0.5),
    }


def reference(x_prompt, x_sample, c_prompt, c_sample, w_mod, b_mod, norm_mix, norm_ffn, w_in, w_out,
              qk_gain, sink_a, relpos_table, diff_lambda, diff_subln, w_gate_up, w_down):
    weights = (w_mod, b_mod, norm_mix, norm_ffn, w_in, w_out, qk_gain, sink_a, relpos_table,
               diff_lambda, diff_subln, w_gate_up, w_down)
    y_prompt = encoder_trunk(x_prompt, c_prompt, *weights)
    y_sample = encoder_trunk(x_sample, c_sample, *weights)
    return (y_prompt, y_sample)
```

```python
import math
import numpy as np
import ml_dtypes
from contextlib import ExitStack
import concourse.bass as bass
import concourse.mybir as mybir
from concourse.bass_utils import run_bass_kernel_spmd

F32 = mybir.dt.float32
BF16 = mybir.dt.bfloat16
AF = mybir.ActivationFunctionType
ALU = mybir.AluOpType

D = 2048
NCH = 16
HD = 64
FF = 5632
NFF = 44
INW = 4608
NEG = -30000.0
EPS = 1e-6
OFF = dict(qa=0, ka=512, va=640, qb=768, kb=1280, vb=1792, qc=2304, kc=2816, vc=3328, qd=3840, kd=4352, vd=4480)
WID = dict(qa=512, ka=128, va=128, qb=512, kb=512, vb=512, qc=512, kc=512, vc=512, qd=512, kd=128, vd=128)
VOFF = dict(va=0, vb=128, vc=640, vd=1152)
GIDX = dict(qa=0, ka=1, qb=2, kb=3, qc=4, kc=5, qd=6, kd=7)
SLAB = dict(a=(128, 384), b=(1408, 2944), c=(768, 1664))
DMA_K = 8
DEBUG_IDS = set()
import os as _os0
KSKIP = _os0.environ.get('KSKIP', '').split(',')


def relpos_bucket_np(rel):
    n = np.abs(rel)
    nf = np.maximum(n, 1).astype(np.float32)
    large = 8 + (np.log(nf / np.float32(8)) / np.float32(math.log(128.0)) * np.float32(8)).astype(np.int32)
    large = np.minimum(large, 15)
    return np.where(rel > 0, 16, 0) + np.where(n < 8, n, large)


class Sched:
    ENG = ["pe", "act", "dve", "pool", "sp"]

    def __init__(self):
        self.ops = {e: [] for e in self.ENG}
        self.cnt = {e: 0 for e in self.ENG}
        self.dman = {e: 0 for e in self.ENG}
        self.waited = {}
        self.last_w = {}
        self.readers = {}
        self.sems = {}
        self.n_instr = 0

    def _wait(self, E, tok):
        key, val = tok
        if key == ("c", E) and E == "pe":
            return
        if self.waited.get((E, key), 0) >= val:
            return
        self.waited[(E, key)] = val
        self.ops[E].append(("w", key, val))

    def _deps(self, reads, writes):
        deps = []
        for b in reads:
            t = self.last_w.get(b)
            if t is not None:
                deps.append(t)
        for b in writes:
            t = self.last_w.get(b)
            if t is not None:
                deps.append(t)
            for k, v in self.readers.get(b, {}).items():
                deps.append((k, v))
        return deps

    def _commit(self, tok, reads, writes):
        for b in writes:
            self.last_w[b] = tok
            self.readers[b] = {}
        for b in reads:
            r = self.readers.setdefault(b, {})
            if r.get(tok[0], 0) < tok[1]:
                r[tok[0]] = tok[1]

    def op(self, E, fn, reads=(), writes=()):
        for t in self._deps(reads, writes):
            self._wait(E, t)
        self.cnt[E] += 1
        tok = (("c", E), self.cnt[E])
        self.ops[E].append(("i", fn, ("c", E), 1))
        self._commit(tok, reads, writes)
        self.n_instr += 1

    def group(self, E, fns, reads=(), writes=()):
        for t in self._deps(reads, writes):
            self._wait(E, t)
        for fn in fns[:-1]:
            self.ops[E].append(("i", fn, None, 0))
        self.cnt[E] += 1
        tok = (("c", E), self.cnt[E])
        self.ops[E].append(("i", fns[-1], ("c", E), 1))
        self._commit(tok, reads, writes)
        self.n_instr += len(fns)

    def dma(self, E, fn, reads=(), writes=()):
        i = self.dman[E]
        self.dman[E] += 1
        key = ("d", E, i % DMA_K)
        if i >= DMA_K:
            self._wait(E, (key, 16 * (i // DMA_K)))
        for t in self._deps(reads, writes):
            self._wait(E, t)
        tok = (key, 16 * (i // DMA_K + 1))
        self.ops[E].append(("i", fn, key, 16))
        self._commit(tok, reads, writes)
        self.n_instr += 1

    def final_wait_all(self, E):
        for e in self.ENG:
            for s in range(DMA_K):
                n = len([1 for j in range(self.dman[e]) if j % DMA_K == s])
                if n:
                    self._wait(E, (("d", e, s), 16 * n))
        for e in self.ENG:
            if e != E and self.cnt[e]:
                self._wait(E, (("c", e), self.cnt[e]))

    def replay(self, nc, stack):
        keys = set()
        for e in self.ENG:
            for o in self.ops[e]:
                if o[0] == "w":
                    keys.add(o[1])
                elif o[2] is not None:
                    keys.add(o[2])
        for k in sorted(keys, key=str):
            self.sems[k] = stack.enter_context(nc.semaphore("s_" + "_".join(str(x) for x in k)))
        block = stack.enter_context(nc.Block())
        sems = self.sems

        def run(name):
            def f(eng):
                for o in self.ops[name]:
                    if o[0] == "w":
                        eng.wait_ge(sems[o[1]], o[2])
                    else:
                        ins = o[1](eng)
                        if DEBUG_IDS:
                            nm = str(getattr(getattr(ins, "ins", ins), "name", ""))
                            if nm in DEBUG_IDS:
                                print("DEBUGID", nm, name, getattr(o[1], "desc", "?"), flush=True)
                        if o[2] is not None:
                            ins.then_inc(sems[o[2]], o[3])
            return f
        block.tensor(run("pe"))
        block.scalar(run("act"))
        block.vector(run("dve"))
        block.gpsimd(run("pool"))
        block.sync(run("sp"))


def host_consts(TV, SEG, pair):
    c = {}
    c["ident_f"] = np.eye(128, dtype=np.float32)
    c["jflip"] = np.eye(128, dtype=np.float32)[::-1].copy().astype(ml_dtypes.bfloat16)
    bo = np.zeros((128, 128), np.float32)
    bo[:64, :64] = 1.0 / 64
    bo[64:, 64:] = 1.0 / 64
    c["blk64"] = bo.astype(ml_dtypes.bfloat16)
    c["onesD"] = np.full((128, 128), 1.0 / D, np.float32).astype(ml_dtypes.bfloat16)
    c["ones128f"] = np.full((128, 128), 1.0 / 128, np.float32)
    c["ones_f"] = np.ones((128, 128), np.float32)
    rm = np.zeros((128, 128), np.float32)
    for hh in range(2):
        for a in range(2):
            for p in range(16):
                i0 = hh * 64 + a * 32 + p
                i1 = i0 + 16
                rm[i1, i0] = -1.0
                rm[i0, i1] = 1.0
    c["rmT"] = rm.astype(ml_dtypes.bfloat16)
    t = np.arange(TV)
    pos = (t % SEG) if pair else t
    rows = (pos // 64).astype(np.float32)
    cols = (pos % 64).astype(np.float32)
    inv = (np.float32(10000.0) ** (-np.arange(16, dtype=np.float32) / np.float32(16))).astype(np.float32)
    cosT = np.zeros((64, TV), np.float32)
    sinT = np.zeros((64, TV), np.float32)
    for a, pp in enumerate((rows, cols)):
        ang = (pp[None, :] * inv[:, None]).astype(np.float32)
        for half in range(2):
            cosT[a * 32 + half * 16: a * 32 + half * 16 + 16] = np.cos(ang)
            sinT[a * 32 + half * 16: a * 32 + half * 16 + 16] = np.sin(ang)
    c["cosT"] = np.concatenate([cosT, cosT], 0)
    c["sinT"] = np.concatenate([sinT, sinT], 0)
    for m in "abc":
        C0, Mw = SLAB[m]
        L = Mw + 128
        rel = C0 + 127 - np.arange(L)
        b = relpos_bucket_np(rel.astype(np.int32))
        oh = np.zeros((32, L), np.float32)
        oh[b, np.arange(L)] = 1.0
        if m == "a":
            mult = (np.abs(rel) <= 128).astype(np.float32)
        elif m == "b":
            mult = np.zeros(L, np.float32)
            for r in (1, 4, 16):
                mult += ((rel % r == 0) & (np.abs(rel) <= 64 * r)).astype(np.float32)
        else:
            mult = np.ones(L, np.float32)
        c["oh_" + m] = oh
        c["mult_" + m] = np.tile(mult[None, :], (8, 1)).astype(np.float32)
    c["xseg"] = np.full((128, 1), NEG if pair else 0.0, np.float32)
    return c


CONST_SPECS = None


def build(TV, SEG, DEPTH):
    NT = TV // 128
    G = 512
    NG = TV // G
    nc = bass.Bass("TRN2", target_bir_lowering=False)
    S = Sched()
    st = ExitStack()

    def din(name, shape, dt=F32):
        return nc.dram_tensor(name, list(shape), dt, kind="ExternalInput").ap()

    def dscr(name, shape, dt):
        dbg = _os0.environ.get("KDEBUG") and (name.startswith("qT_") or name.startswith("kT_") or name in ("v_tm", "omixT"))
        return nc.dram_tensor(name, list(shape), dt, kind="ExternalOutput" if dbg else "Internal").ap()

    x_in = din("x", [TV, D])
    y_out = nc.dram_tensor("y", [TV, D], F32, kind="ExternalOutput").ap()
    c2 = din("c2", [2, D])
    w_mod = din("w_mod", [DEPTH, D, 6 * D])
    b_mod = din("b_mod", [DEPTH, 6 * D])
    norm_mix = din("norm_mix", [DEPTH, D])
    norm_ffn = din("norm_ffn", [DEPTH, D])
    w_in = din("w_in", [DEPTH, D, INW])
    w_out = din("w_out", [DEPTH, D, D])
    qk_gain = din("qk_gain", [DEPTH, 4, 2, HD])
    sink_a = din("sink_a", [DEPTH, 8])
    relpos = din("relpos_table", [32, 20])
    diff_lambda = din("diff_lambda", [DEPTH, 4, HD])
    diff_subln = din("diff_subln", [DEPTH, 128])
    w_gu = din("w_gate_up", [DEPTH, D, 2 * FF])
    w_down = din("w_down", [DEPTH, FF, D])
    hc = host_consts(TV, SEG, False)
    cin = {}
    for k, v in hc.items():
        cin[k] = din("k_" + k, v.shape, BF16 if v.dtype == ml_dtypes.bfloat16 else F32)

    xT = dscr("xT", [NCH, 128, TV], F32)
    wbf_in = dscr("wbf_in", [DEPTH, D, INW], BF16)
    wbf_out = dscr("wbf_out", [DEPTH, D, D], BF16)
    wbf_gu = dscr("wbf_gu", [DEPTH, D, 2 * FF], BF16)
    wbf_down = dscr("wbf_down", [DEPTH, FF, D], BF16)
    qTd = {m: dscr("qT_" + m, [512, TV], BF16) for m in "abcd"}
    kTd = {"a": dscr("kT_a", [128, TV], BF16), "b": dscr("kT_b", [512, TV], BF16),
           "c": dscr("kT_c", [512, TV], BF16), "d": dscr("kT_d", [128, TV], BF16)}
    v_tm = dscr("v_tm", [TV, 1280], BF16)
    omixT = dscr("omixT", [D, TV], BF16)
    gvec = {m: dscr("gvec_" + m, [8, SLAB[m][1] + 128], BF16) for m in "abc"}

    def sb(name, shape, dt):
        return st.enter_context(nc.sbuf_tensor(name, list(shape), dt))

    slabd = {m: dscr("slabd_" + m, [8, 128, SLAB[m][1]], BF16) for m in "bc"}
    ps = [st.enter_context(nc.psum_tensor(f"ps{i}", [128, 512], F32)) for i in range(8)]
    PSB = [("ps", i) for i in range(8)]
    ident_f = sb("ident_f", [128, 128], F32)
    jflip = sb("jflip", [128, 128], BF16)
    blk64 = sb("blk64", [128, 128], BF16)
    onesD = sb("onesD", [128, 128], BF16)
    ones128f = sb("ones128f", [128, 128], F32)
    ones_f = sb("ones_f", [128, 128], F32)
    rmT = sb("rmT", [128, 128], BF16)
    xseg = sb("xseg", [128, 1], F32)
    slab_a = sb("slab_a", [128, 8, SLAB["a"][1]], BF16)
    slabcur = {"b": sb("slabcur_b", [128, SLAB["b"][1]], BF16), "c": sb("slabcur_c", [128, SLAB["c"][1]], BF16)}
    WA = sb("WA", [128, 2, 3, 512], BF16)
    modsb = sb("modsb", [128, 96, 2], F32)
    amod = sb("amod", [128, 2, 16, 2], F32)
    nrm = sb("nrm", [128, 2, 16], F32)
    bmodsb = sb("bmodsb", [128, 96], F32)
    siluc = sb("siluc", [128, 2, 16], BF16)
    c2sb = sb("c2sb", [128, 32], F32)
    gains = sb("gains", [128, 8], F32)
    sinkrow = sb("sinkrow", [128, 2, 512], F32)
    sinkraw = sb("sinkraw", [128, 8], F32)
    cfar = sb("cfar", [128, 4, 4], F32)
    lam = sb("lam", [128, 8], F32)
    lamv = sb("lamv", [128, 4], F32)
    subg = sb("subg", [128, 1], F32)
    onesb = sb("onesb", [128, 8], BF16)
    ones_bf = sb("ones_bf", [128, 128], BF16)
    xg = sb("xg", [128, NCH, G], F32)
    hT = sb("hT", [128, NCH, G], BF16)
    big = sb("big", [128, 48, G], BF16)
    wblk = [sb(f"wblk{i}", [128, 16, 512], BF16) for i in range(2)]
    tmpf_all = sb("tmpf_all", [128, 4, 512], F32)
    tmpf = [tmpf_all[:, i, :] for i in range(4)]
    xtok = tmpf_all[:, :, :].rearrange("p a b -> p (a b)")
    XT_IDS = [("tf", i) for i in range(4)]
    tmpb_all = sb("tmpb_all", [128, 4, 512], BF16)
    tmpb = [tmpb_all[:, i, :] for i in range(4)]
    qbuf_all = sb("qbuf_all", [128, 2, 512], BF16)
    qbuf = [qbuf_all[:, i, :] for i in range(2)]
    ropec = sb("ropec", [128, 512], F32)
    ropes = sb("ropes", [128, 512], F32)
    bigflat = big[:, :, :].rearrange("p a b -> p (a b)")

    cnt = {"tf": 0, "tb": 0, "w": 0, "ps": 0, "q": 0}

    def rot(kind, n):
        cnt[kind] += 1
        return (cnt[kind] - 1) % n

    def I(name, *a, **kw):
        f = lambda e: getattr(e, name)(*a, **kw)
        f.desc = name + " " + str([str(x)[:80] for x in a])
        return f

    def dma(out, in_, reads, writes, eng="sp", slow=False):
        if slow:
            S.dma(eng, lambda e, o=out, i=in_: e.dma_start(out=o, in_=i, allow_slow_non_contiguous=True), reads, writes)
        else:
            S.dma(eng, lambda e, o=out, i=in_: e.dma_start(out=o, in_=i), reads, writes)

    def load_T(dst, src, n, reads, writes, p=128):
        dma(tmpf[0][0:n, 0:p], src, reads, [("tf", 0)])
        S.group("pe", [I("transpose", ps[7][0:p, 0:n], tmpf[0][0:n, 0:p], ident_f[0:n, 0:n])], [("tf", 0), "ident_f"], [PSB[7]])
        S.op("dve", I("tensor_copy", dst, ps[7][0:p, 0:n]), [PSB[7]], writes)

    for t_, n_ in ((ident_f, "ident_f"), (jflip, "jflip"), (blk64, "blk64"), (onesD, "onesD"),
                   (ones128f, "ones128f"), (ones_f, "ones_f"), (rmT, "rmT"), (xseg, "xseg")):
        dma(t_[:, :], cin[n_][:, :], [], [n_])
    S.op("pool", I("memset", onesb[:, :], 1.0), [], ["onesb"])
    S.op("pool", I("memset", ones_bf[:, :], 1.0), [], ["ones_bf"])

    epsb = sb("epsb", [128, 1], F32)
    S.op("pool", I("memset", epsb[:, :], EPS), [], ["epsb"])

    def rstd_op(dst, src, dst_id, src_id):
        np_ = dst.shape[0]
        S.op("act", I("activation", dst, src, AF.Sqrt, bias=epsb[0:np_, 0:1], scale=1.0), [src_id, "epsb"], [dst_id])
        S.op("dve", I("reciprocal", dst, dst), [dst_id], [dst_id])

    def mm_group(out_ap, pairs, reads, writes):
        n = len(pairs)
        fns = [I("matmul", out_ap, l, r, start=(i == 0), stop=(i == n - 1)) for i, (l, r) in enumerate(pairs)]
        S.group("pe", fns, reads, writes)

    def cast_weights(l):
        for src, dst, rows, name in ((w_in, wbf_in, D, "win"), (w_out, wbf_out, D, "wout"),
                                     (w_gu, wbf_gu, D, "wgu"), (w_down, wbf_down, FF, "wdown")):
            step = 256
            for r0 in range(0, rows, step):
                dma(dst[l, r0:r0 + step, :], src[l, r0:r0 + step, :], [], [(name, l)], eng="pool")

    def build_slabs():
        tab = tmpf[0]
        dma(tab[0:32, 0:20], relpos[:, :], [], [("tf", 0)])
        for m, col0, nh in (("a", 0, 8), ("b", 8, 8), ("c", 16, 4)):
            C0, Mw = SLAB[m]
            L = Mw + 128
            for c0 in range(0, L, 512):
                w = min(512, L - c0)
                oh, mu, ex, gb = tmpf[1], tmpf[2], tmpf[3], tmpb[0]
                dma(oh[0:32, 0:w], cin["oh_" + m][:, c0:c0 + w], [], [("tf", 1)])
                dma(mu[0:8, 0:w], cin["mult_" + m][:, c0:c0 + w], [], [("tf", 2)])
                mm_group(ps[0][0:nh, 0:w], [(tab[0:32, col0:col0 + nh], oh[0:32, 0:w])], [("tf", 0), ("tf", 1)], [PSB[0]])
                S.op("act", I("activation", ex[0:nh, 0:w], ps[0][0:nh, 0:w], AF.Exp), [PSB[0]], [("tf", 3)])
                S.op("dve", I("tensor_tensor", gb[0:nh, 0:w], ex[0:nh, 0:w], mu[0:nh, 0:w], ALU.mult),
                     [("tf", 3), ("tf", 2)], [("tb", 0)])
                dma(gvec[m][0:nh, c0:c0 + w], gb[0:nh, 0:w], [("tb", 0)], [("gvec", m)])
            for h in range(nh):
                for c0 in range(0, Mw, 512):
                    w = min(512, Mw - c0)
                    hi = 1 + (h + c0 // 512) % 2
                    hk = tmpb[hi]
                    src = bass.AP(gvec[m].tensor, h * L + c0, [[1, 128], [1, w]])
                    dma(hk[:, 0:w], src, [("gvec", m)], [("tb", hi)])
                    mm_group(ps[hi][:, 0:w], [(jflip[:, :], hk[:, 0:w])], [("tb", hi), "jflip"], [PSB[hi]])
                    if m == "a":
                        S.op("act", I("activation", slab_a[:, h, c0:c0 + w], ps[hi][:, 0:w], AF.Copy), [PSB[hi]], ["slab_a"])
                    else:
                        S.op("act", I("activation", tmpb[3][:, 0:w], ps[hi][:, 0:w], AF.Copy), [PSB[hi]], [("tb", 3)])
                        dma(slabd[m][h, :, c0:c0 + w], tmpb[3][:, 0:w], [("tb", 3)], [("slabd", m)])
        for kv in range(2):
            for di, dl in enumerate((-1, 0, 1)):
                st0 = 128 - 128 * dl
                for j in range(4):
                    S.op("dve", I("tensor_copy", WA[:, kv, di, j * 128:(j + 1) * 128], slab_a[:, kv * 4 + j, st0:st0 + 128]),
                         ["slab_a"], ["WA"])
        dma(tmpf[1][0:1, 0:4], relpos[15:16, 16:20], [], [("tf", 1)])
        dma(tmpf[1][0:1, 4:8], relpos[31:32, 16:20], [], [("tf", 1)])
        mm_group(ps[7][:, 0:8], [(ones_f[0:1, :], tmpf[1][0:1, 0:8])], [("tf", 1), "ones_f"], [PSB[7]])
        for col in (0, 1):
            S.op("dve", I("tensor_copy", cfar[:, :, col], ps[7][:, col * 4:(col + 1) * 4]), [PSB[7]], ["cfar"])
        for col in (0, 1):
            for h in range(4):
                S.op("dve", I("tensor_tensor", cfar[:, h, col + 2:col + 3], cfar[:, h, col:col + 1], xseg[:, 0:1], ALU.add),
                     ["cfar", "xseg"], ["cfar"])

    def x_to_xT():
        for g in range(NG):
            for tt in range(4):
                r0 = g * G + tt * 128
                dma(xtok, x_in[r0:r0 + 128, :], [], XT_IDS)
                for c4 in range(4):
                    pb = rot("ps", 4)
                    fns = [I("transpose", ps[pb][:, j * 128:(j + 1) * 128],
                             xtok[:, (c4 * 4 + j) * 128:(c4 * 4 + j + 1) * 128], ident_f[:, :]) for j in range(4)]
                    S.group("pe", fns, XT_IDS + ["ident_f"], [PSB[pb]])
                    dst = xg[:, c4 * 4:c4 * 4 + 4, tt * 128:(tt + 1) * 128]
                    srcp = ps[pb][:, :].rearrange("p (j t) -> p j t", j=4)
                    if c4 % 2:
                        S.op("act", I("activation", dst, srcp, AF.Copy), [PSB[pb]], ["xg"])
                    else:
                        S.op("dve", I("tensor_copy", dst, srcp), [PSB[pb]], ["xg"])
            dma(xT[:, :, g * G:(g + 1) * G].rearrange("c p t -> p c t"), xg[:, :, :], ["xg"], [("xT", g)])

    def xT_to_y():
        for g in range(NG):
            dma(xg[:, :, :], xT[:, :, g * G:(g + 1) * G].rearrange("c p t -> p c t"), [("xT", g)], ["xg"])
            for tt in range(4):
                for c4 in range(4):
                    pb = rot("ps", 4)
                    fns = [I("transpose", ps[pb][:, j * 128:(j + 1) * 128],
                             xg[:, c4 * 4 + j, tt * 128:(tt + 1) * 128], ident_f[:, :]) for j in range(4)]
                    S.group("pe", fns, ["xg", "ident_f"], [PSB[pb]])
                    if c4 % 2:
                        S.op("act", I("activation", xtok[:, c4 * 512:(c4 + 1) * 512], ps[pb][:, :], AF.Copy), [PSB[pb]], XT_IDS)
                    else:
                        S.op("dve", I("tensor_copy", xtok[:, c4 * 512:(c4 + 1) * 512], ps[pb][:, :]), [PSB[pb]], XT_IDS)
                r0 = g * G + tt * 128
                dma(y_out[r0:r0 + 128, :], xtok, XT_IDS, ["y"])

    def layer_params(l):
        if l == 0:
            load_T(c2sb[:, 0:32], c2.rearrange("s (c p) -> (s c) p", p=128), 32, [], ["c2sb"])
            S.op("act", I("activation", siluc[:, :, :].rearrange("p s c -> p (s c)"), c2sb[:, 0:32], AF.Silu), ["c2sb"], ["siluc"])
        load_T(bmodsb[:, 0:96], b_mod[l].rearrange("(c p) -> c p", p=128), 96, [], ["bmodsb"])
        load_T(nrm[:, 0, :], norm_mix[l].rearrange("(c p) -> c p", p=128), 16, [], ["nrm"])
        load_T(nrm[:, 1, :], norm_ffn[l].rearrange("(c p) -> c p", p=128), 16, [], ["nrm"])
        for blk in range(24):
            wb = rot("w", 2)
            dma(wblk[wb][:, :, :], w_mod[l, :, blk * 512:(blk + 1) * 512].rearrange("(k p) n -> p k n", p=128),
                [], [("wblk", wb)], eng="pool")
            for j in range(4):
                fc = blk * 4 + j
                for s in range(2):
                    mm_group(ps[7][:, j * 2 + s:j * 2 + s + 1],
                             [(wblk[wb][:, k, j * 128:(j + 1) * 128], siluc[:, s, k:k + 1]) for k in range(16)],
                             [("wblk", wb), "siluc"], [PSB[7]])
                S.op("dve", I("tensor_scalar", modsb[:, fc, :], ps[7][:, j * 2:j * 2 + 2], bmodsb[:, fc:fc + 1], None, ALU.add),
                     [PSB[7], "bmodsb"], ["modsb"])
        for t, sc0 in ((0, 16), (1, 64)):
            for s in range(2):
                S.op("dve", I("scalar_tensor_tensor", amod[:, t, :, s], modsb[:, sc0:sc0 + 16, s], 1.0, nrm[:, t, :], ALU.add, ALU.mult),
                     ["modsb", "nrm"], ["amod"])
        for hh in range(2):
            dma(gains[hh * 64:(hh + 1) * 64, 0:8], qk_gain[l].rearrange("m t d -> d (m t)"), [], ["gains"], slow=True)
        for m in range(4):
            S.op("dve", I("tensor_scalar", gains[:, 2 * m:2 * m + 1], gains[:, 2 * m:2 * m + 1], HD ** -0.5, None, ALU.mult),
                 ["gains"], ["gains"])
        dma(sinkraw[64:65, :], sink_a[l:l + 1, :], [], ["sinkraw"])
        S.op("act", I("activation", sinkraw[64:65, :], sinkraw[64:65, :], AF.Exp), ["sinkraw"], ["sinkraw"])
        S.op("pool", I("memset", sinkrow[64:65, :, :], 0.0), [], ["sinkrow"])
        for kv in range(2):
            for j in range(4):
                S.op("dve", I("tensor_scalar", sinkrow[64:65, kv, j * 128:(j + 1) * 128], sinkrow[64:65, kv, j * 128:(j + 1) * 128],
                              sinkraw[64:65, kv * 4 + j:kv * 4 + j + 1], None, ALU.add), ["sinkraw", "sinkrow"], ["sinkrow"])
        lam_init = 0.8 - 0.6 * math.exp(-0.3 * l)
        dma(lamv[0:64, 0:4], diff_lambda[l].rearrange("j d -> d j"), [], ["lamv"], slow=True)
        for i in range(2):
            S.op("dve", I("tensor_tensor", lamv[0:64, 2 * i:2 * i + 1], lamv[0:64, 2 * i:2 * i + 1], lamv[0:64, 2 * i + 1:2 * i + 2], ALU.mult),
                 ["lamv"], ["lamv"])
        for i in range(2):
            mm_group(ps[7][:, i:i + 1], [(ones_f[0:64, :], lamv[0:64, 2 * i:2 * i + 1])], ["lamv", "ones_f"], [PSB[7]])
        S.op("act", I("activation", lam[:, 0:2], ps[7][:, 0:2], AF.Exp), [PSB[7]], ["lam"])
        S.op("dve", I("tensor_tensor", lam[:, 2:3], lam[:, 1:2], lam[:, 0:1], ALU.subtract), ["lam"], ["lam"])
        S.op("dve", I("tensor_scalar", lam[:, 3:4], lam[:, 2:3], -lam_init, None, ALU.add), ["lam"], ["lam"])
        dma(subg[:, 0:1], diff_subln[l].rearrange("(p o) -> p o", o=1), [], ["subg"])
        S.op("dve", I("tensor_scalar", subg[:, 0:1], subg[:, 0:1], 1.0 - lam_init, None, ALU.mult), ["subg"], ["subg"])

    def make_hT(t, g):
        s = (g * G) // SEG
        sh0 = 0 if t == 0 else 48
        S.op("act", I("activation", big[:, 0:NCH, :], xg[:, :, :], AF.Square), ["xg"], ["big"])
        mm_group(ps[6][:, :], [(onesD[:, :], big[:, k, :]) for k in range(NCH)], ["big", "onesD"], [PSB[6]])
        rstd = tmpf[3]
        rstd_op(rstd, ps[6][:, :], ("tf", 3), PSB[6])
        for c in range(NCH):
            i = rot("tf", 3)
            S.op("dve", I("scalar_tensor_tensor", tmpf[i], xg[:, c, :], amod[:, t, c, s:s + 1], rstd, ALU.mult, ALU.mult),
                 ["xg", "amod", ("tf", 3)], [("tf", i)])
            S.op("pool", I("tensor_scalar", hT[:, c, :], tmpf[i], modsb[:, sh0 + c, s:s + 1], None, ALU.add),
                 [("tf", i), "modsb"], ["hT"])

    def phase1(l):
        for g in range(NG):
            t0 = g * G
            dma(xg[:, :, :], xT[:, :, t0:t0 + G].rearrange("c p t -> p c t"), [("xT", g)], ["xg"])
            make_hT(0, g)
            dma(ropec[:, :], cin["cosT"][:, t0:t0 + G], [], ["ropec"])
            dma(ropes[:, :], cin["sinT"][:, t0:t0 + G], [], ["ropes"])
            for name in ("qa", "ka", "qb", "kb", "qc", "kc", "qd", "kd"):
                wdt = WID[name]
                wb = rot("w", 2)
                dma(wblk[wb][:, :, 0:wdt], wbf_in[l, :, OFF[name]:OFF[name] + wdt].rearrange("(k p) n -> p k n", p=128),
                    [("win", l)], [("wblk", wb)])
                mix = name[1]
                dst = qTd[mix] if name[0] == "q" else kTd[mix]
                gcol = GIDX[name]
                for j in range(wdt // 128):
                    pb = rot("ps", 4)
                    mm_group(ps[pb][:, :], [(wblk[wb][:, k, j * 128:(j + 1) * 128], hT[:, k, :]) for k in range(NCH)],
                             [("wblk", wb), "hT"], [PSB[pb]])
                    ib = rot("tb", 4)
                    S.op("act", I("activation", tmpb[ib], ps[pb][:, :], AF.Square), [PSB[pb]], [("tb", ib)])
                    mm_group(ps[4][:, :], [(blk64[:, :], tmpb[ib])], [("tb", ib), "blk64"], [PSB[4]])
                    i2 = rot("tf", 3)
                    rstd_op(tmpf[i2], ps[4][:, :], ("tf", i2), PSB[4])
                    ob = rot("tb", 4)
                    S.op("dve", I("scalar_tensor_tensor", tmpb[ob], ps[pb][:, :], gains[:, gcol:gcol + 1], tmpf[i2], ALU.mult, ALU.mult),
                         [PSB[pb], ("tf", i2), "gains"], [("tb", ob)])
                    if mix == "d":
                        mm_group(ps[5][:, :], [(rmT[:, :], tmpb[ob])], [("tb", ob), "rmT"], [PSB[5]])
                        i3 = rot("tf", 3)
                        S.op("pool", I("tensor_tensor", tmpf[i3], tmpb[ob], ropec[:, :], ALU.mult), [("tb", ob), "ropec"], [("tf", i3)])
                        i4 = rot("tf", 3)
                        S.op("dve", I("tensor_tensor", tmpf[i4], ps[5][:, :], ropes[:, :], ALU.mult), [PSB[5], "ropes"], [("tf", i4)])
                        ob2 = rot("tb", 4)
                        S.op("pool", I("tensor_tensor", tmpb[ob2], tmpf[i3], tmpf[i4], ALU.add), [("tf", i3), ("tf", i4)], [("tb", ob2)])
                        ob = ob2
                    dma(dst[j * 128:(j + 1) * 128, t0:t0 + G], tmpb[ob], [("tb", ob)], [(name, g)])
            for name in ("va", "vb", "vc", "vd"):
                wdt = WID[name]
                wb = rot("w", 2)
                dma(wblk[wb][:, :, 0:wdt], wbf_in[l, :, OFF[name]:OFF[name] + wdt].rearrange("(k p) n -> p k n", p=128),
                    [("win", l)], [("wblk", wb)])
                for tt in range(4):
                    pb = rot("ps", 4)
                    mm_group(ps[pb][:, 0:wdt], [(hT[:, k, tt * 128:(tt + 1) * 128], wblk[wb][:, k, 0:wdt]) for k in range(NCH)],
                             [("wblk", wb), "hT"], [PSB[pb]])
                    ob = rot("tb", 4)
                    S.op("act", I("activation", tmpb[ob][:, 0:wdt], ps[pb][:, 0:wdt], AF.Copy), [PSB[pb]], [("tb", ob)])
                    r0 = t0 + tt * 128
                    dma(v_tm[r0:r0 + 128, VOFF[name]:VOFF[name] + wdt], tmpb[ob][:, 0:wdt], [("tb", ob)], [("v", g)])

    def attn_qgroup(mix, kv, qi, NQ, vview, M, OB):
        DB = 6
        q0 = qi * NQ
        qg = q0 // G
        qb_i = rot("q", 2)
        qsb = qbuf[qb_i]
        QID = ("qbuf", qb_i)
        if mix in ("a", "d"):
            src = qTd[mix][kv * 256:(kv + 1) * 256, q0:q0 + 128].rearrange("(j d) c -> d j c", j=4)
            dma(qsb[0:64, :].rearrange("d (j c) -> d j c", j=4), src, [("q" + mix, qg)], [QID])
        elif mix == "b":
            dma(qsb[0:64, :], qTd["b"][kv * 64:(kv + 1) * 64, q0:q0 + 512], [("qb", qg)], [QID])
        else:
            for m in range(2):
                dma(qsb[0:64, m * 256:(m + 1) * 256], qTd["c"][kv * 128 + m * 64:kv * 128 + (m + 1) * 64, q0:q0 + 256], [("qc", qg)], [QID])
        tiles = []
        qseg = q0 // SEG
        for kt in range(NT):
            k0 = kt * 128
            dk = k0 - q0
            cross = (k0 // SEG) != qseg
            xb = xseg[:, 0:1] if cross else None
            if mix == "a":
                if abs(dk) <= 128:
                    tiles.append((kt, xb, ("A", dk // 128 + 1)))
            elif mix == "b":
                if -1024 <= dk <= 1408:
                    tiles.append((kt, xb, ("S", q0 - k0 + 1408)))
            elif mix == "c":
                if -640 <= dk <= 768:
                    tiles.append((kt, xb, ("S", q0 - k0 + 768)))
                else:
                    col = (1 if dk > 0 else 0) + (2 if cross else 0)
                    tiles.append((kt, cfar[:, kv, col:col + 1], None))
            else:
                tiles.append((kt, xb, None))
        nt_ = len(tiles)
        for ti, (kt, bias, wspec) in enumerate(tiles):
            sb_ = rot("ps", 4)
            if mix == "c":
                fns = [I("matmul", ps[sb_][:, m * 256:(m + 1) * 256], bigflat[0:64, m * TV + kt * 128:m * TV + (kt + 1) * 128],
                         qsb[0:64, m * 256:(m + 1) * 256], start=True, stop=True) for m in range(2)]
                S.group("pe", fns, ["big", QID], [PSB[sb_]])
            else:
                mm_group(ps[sb_][:, :], [(bigflat[0:64, kt * 128:(kt + 1) * 128], qsb[0:64, :])], ["big", QID], [PSB[sb_]])
            pi = rot("tb", 4)
            P = tmpb[pi]
            if bias is None:
                S.op("act", I("activation", P, ps[sb_][:, :], AF.Exp), [PSB[sb_]], [("tb", pi)])
            else:
                S.op("act", I("activation", P, ps[sb_][:, :], AF.Exp, bias=bias), [PSB[sb_], "cfar", "xseg"], [("tb", pi)])
            if wspec is not None:
                weng = "dve" if (ti % 2 == 0) else "pool"
                if wspec[0] == "A":
                    S.op(weng, I("tensor_tensor", P, P, WA[:, kv, wspec[1], :], ALU.mult), [("tb", pi), "WA"], [("tb", pi)])
                elif mix == "b":
                    s0 = wspec[1]
                    S.op(weng, I("tensor_tensor", P, P, slabcur["b"][:, s0:s0 + 512], ALU.mult), [("tb", pi), ("slabcur", "b")], [("tb", pi)])
                else:
                    s0 = wspec[1]
                    for m in range(2):
                        S.op("dve" if m == 0 else "pool",
                             I("tensor_tensor", P[:, m * 256:(m + 1) * 256], P[:, m * 256:(m + 1) * 256], slabcur["c"][:, s0:s0 + 256], ALU.mult),
                             [("tb", pi), ("slabcur", "c")], [("tb", pi)])
            first, last = (ti == 0), (ti == nt_ - 1)
            if mix == "c":
                fns = [I("matmul", ps[4 + m][:, 0:256], vview[:, kt, 0:128], P[:, m * 256:(m + 1) * 256],
                         start=first, stop=last) for m in range(2)]
                fns.append(I("matmul", ps[DB][:, :], ones_bf[:, :], P, start=first, stop=last))
                S.group("pe", fns, ["big", ("tb", pi), "ones_bf"], [PSB[4], PSB[5], PSB[DB]])
            else:
                S.group("pe", [I("matmul", ps[OB][0:65, :], vview[:, kt, 0:65], P, start=first, stop=last)],
                        ["big", ("tb", pi)], [PSB[OB]])
        if mix == "c" and "fin" in KSKIP:
            return
        if mix != "c":
            i1 = rot("tf", 3)
            r = tmpf[i1]
            if mix == "a":
                S.op("dve", I("tensor_tensor", r[64:65, :], ps[OB][64:65, :], sinkrow[64:65, kv, :], ALU.add), [PSB[OB], "sinkrow"], [("tf", i1)])
                S.op("dve", I("reciprocal", r[64:65, :], r[64:65, :]), [("tf", i1)], [("tf", i1)])
            else:
                S.op("dve", I("reciprocal", r[64:65, :], ps[OB][64:65, :]), [PSB[OB]], [("tf", i1)])
            mm_group(ps[7][0:64, :], [(ones_f[64:65, 0:64], r[64:65, :])], [("tf", i1), "ones_f"], [PSB[7]])
            i2 = rot("tf", 3)
            osb = tmpf[i2]
            S.op("act", I("activation", osb[0:64, :], ps[OB][0:64, :], AF.Copy), [PSB[OB]], [("tf", i2)])
            oi = rot("tb", 4)
            on = tmpb[oi]
            S.op("dve", I("tensor_tensor", on[0:64, :], osb[0:64, :], ps[7][0:64, :], ALU.mult), [("tf", i2), PSB[7]], [("tb", oi)])
            mrow = {"a": 0, "b": 512, "d": 1536}[mix]
            if mix == "b":
                dma(omixT[mrow + kv * 64:mrow + (kv + 1) * 64, q0:q0 + 512], on[0:64, :], [("tb", oi)], [("omix", qg)])
            else:
                dst = omixT[mrow + kv * 256:mrow + (kv + 1) * 256, q0:q0 + 128].rearrange("(j d) c -> d j c", j=4)
                dma(dst, on[0:64, :].rearrange("d (j c) -> d j c", j=4), [("tb", oi)], [("omix", qg)])
        else:
            i1 = rot("tf", 3)
            r = tmpf[i1]
            S.op("dve", I("reciprocal", r, ps[DB][:, :]), [PSB[DB]], [("tf", i1)])
            i2 = rot("tf", 3)
            osb = tmpf[i2]
            S.op("dve", I("tensor_tensor", osb[:, 0:256], ps[4][:, 0:256], r[:, 0:256], ALU.mult), [PSB[4], ("tf", i1)], [("tf", i2)])
            S.op("dve", I("scalar_tensor_tensor", osb[:, 256:512], ps[5][:, 0:256], lam[:, 3:4], r[:, 256:512], ALU.mult, ALU.mult),
                 [PSB[5], ("tf", i1), "lam"], [("tf", i2)])
            i3 = rot("tf", 3)
            av = tmpf[i3]
            S.op("pool", I("tensor_tensor", av[:, 0:256], osb[:, 0:256], osb[:, 256:512], ALU.add), [("tf", i2)], [("tf", i3)])
            S.op("act", I("activation", av[:, 256:512], av[:, 0:256], AF.Square), [("tf", i3)], [("tf", i3)])
            mm_group(ps[7][:, 0:256], [(ones128f[:, :], av[:, 256:512])], [("tf", i3), "ones128f"], [PSB[7]])
            rstd_op(av[:, 256:512], ps[7][:, 0:256], ("tf", i3), PSB[7])
            oi = rot("tb", 4)
            on = tmpb[oi]
            S.op("dve", I("scalar_tensor_tensor", on[:, 0:256], av[:, 0:256], subg[:, 0:1], av[:, 256:512], ALU.mult, ALU.mult),
                 [("tf", i3), "subg"], [("tb", oi)])
            dma(omixT[1024 + kv * 128:1024 + (kv + 1) * 128, q0:q0 + 256], on[:, 0:256], [("tb", oi)], [("omix", qg)])

    def attn_head(mix, kv):
        allg = list(range(NG))
        dv = 128 if mix == "c" else 64
        M = dv if mix == "c" else 65
        if mix == "c":
            for m in range(2):
                dma(bigflat[0:64, m * TV:(m + 1) * TV], kTd["c"][kv * 128 + m * 64:kv * 128 + (m + 1) * 64, :], [("kc", g) for g in allg], ["big"])
        else:
            dma(bigflat[0:64, 0:TV], kTd[mix][kv * 64:(kv + 1) * 64, :], [("k" + mix, g) for g in allg], ["big"])
        vo = 2 * TV if mix == "c" else TV
        vview = bigflat[:, vo:vo + NT * M].rearrange("p (t c) -> p t c", c=M)
        vcol = VOFF["v" + mix] + kv * dv
        for t0 in range(0, NT, 16):
            t1 = min(NT, t0 + 16)
            dma(vview[:, t0:t1, 0:dv], v_tm[t0 * 128:t1 * 128, vcol:vcol + dv].rearrange("(t p) c -> p t c", p=128),
                [("v", g) for g in allg], ["big"])
        if mix != "c":
            S.op("pool", I("memset", vview[:, :, 64:65], 1.0), [], ["big"])
        if mix in ("b", "c"):
            dma(slabcur[mix][:, :], slabd[mix][kv, :, :], [("slabd", mix)], [("slabcur", mix)])
        NQ = {"a": 128, "d": 128, "b": 512, "c": 256}[mix]
        for qi in range(TV // NQ):
            attn_qgroup(mix, kv, qi, NQ, vview, M, 4 + (qi % 2))

    def attention(l):
        import os as _os
        for mix in _os.environ.get("KMIX", "abcd"):
            for kv in range({"a": 2, "b": 8, "c": 4, "d": 2}[mix]):
                attn_head(mix, kv)

    def phase3(l):
        for g in range(NG):
            t0 = g * G
            s = t0 // SEG
            dma(xg[:, :, :], xT[:, :, t0:t0 + G].rearrange("c p t -> p c t"), [("xT", g)], ["xg"])
            dma(hT[:, :, :], omixT[:, t0:t0 + G].rearrange("(c p) t -> p c t", p=128), [("omix", g)], ["hT"])
            for cb in range(4):
                wb = rot("w", 2)
                dma(wblk[wb][:, :, :], wbf_out[l, :, cb * 512:(cb + 1) * 512].rearrange("(k p) n -> p k n", p=128),
                    [("wout", l)], [("wblk", wb)])
                for j in range(4):
                    c = cb * 4 + j
                    pb = rot("ps", 4)
                    mm_group(ps[pb][:, :], [(wblk[wb][:, k, j * 128:(j + 1) * 128], hT[:, k, :]) for k in range(NCH)],
                             [("wblk", wb), "hT"], [PSB[pb]])
                    S.op("dve", I("scalar_tensor_tensor", xg[:, c, :], ps[pb][:, :], modsb[:, 32 + c, s:s + 1], xg[:, c, :], ALU.mult, ALU.add),
                         [PSB[pb], "modsb", "xg"], ["xg"])
            make_hT(1, g)
            for cb in range(FF // 256):
                wb = rot("w", 2)
                dma(wblk[wb][:, :, 0:256], wbf_gu[l, :, cb * 256:(cb + 1) * 256].rearrange("(k p) n -> p k n", p=128),
                    [("wgu", l)], [("wblk", wb)])
                dma(wblk[wb][:, :, 256:512], wbf_gu[l, :, FF + cb * 256:FF + (cb + 1) * 256].rearrange("(k p) n -> p k n", p=128),
                    [("wgu", l)], [("wblk", wb)])
                for j in range(2):
                    fc = cb * 2 + j
                    pg = rot("ps", 4)
                    mm_group(ps[pg][:, :], [(wblk[wb][:, k, j * 128:(j + 1) * 128], hT[:, k, :]) for k in range(NCH)],
                             [("wblk", wb), "hT"], [PSB[pg]])
                    pu = rot("ps", 4)
                    mm_group(ps[pu][:, :], [(wblk[wb][:, k, 256 + j * 128:256 + (j + 1) * 128], hT[:, k, :]) for k in range(NCH)],
                             [("wblk", wb), "hT"], [PSB[pu]])
                    i1 = rot("tf", 3)
                    S.op("act", I("activation", tmpf[i1], ps[pg][:, :], AF.Silu), [PSB[pg]], [("tf", i1)])
                    S.op("dve", I("tensor_tensor", big[:, fc, :], tmpf[i1], ps[pu][:, :], ALU.mult), [("tf", i1), PSB[pu]], ["big"])
            for c in range(NCH):
                wb = rot("w", 2)
                wv = wblk[wb][:, :, :].rearrange("p a b -> p (a b)")[:, 0:NFF * 128].rearrange("p (k n) -> p k n", n=128)
                dma(wv, wbf_down[l, :, c * 128:(c + 1) * 128].rearrange("(k p) n -> p k n", p=128), [("wdown", l)], [("wblk", wb)])
                pb = rot("ps", 4)
                mm_group(ps[pb][:, :], [(wv[:, k, :], big[:, k, :]) for k in range(NFF)], [("wblk", wb), "big"], [PSB[pb]])
                S.op("dve", I("scalar_tensor_tensor", xg[:, c, :], ps[pb][:, :], modsb[:, 80 + c, s:s + 1], xg[:, c, :], ALU.mult, ALU.add),
                     [PSB[pb], "modsb", "xg"], ["xg"])
            dma(xT[:, :, t0:t0 + G].rearrange("c p t -> p c t"), xg[:, :, :], ["xg"], [("xT", g)])

    import os as _os
    PH = _os.environ.get("KPH", "slabs,x2xt,cast,params,p1,attn,p3,y").split(",")
    if "slabs" in PH:
        build_slabs()
    if "x2xt" in PH:
        x_to_xT()
    for l in range(DEPTH):
        if "cast" in PH:
            cast_weights(l)
        if "params" in PH:
            layer_params(l)
        if "p1" in PH:
            phase1(l)
        if "attn" in PH:
            attention(l)
        if "p3" in PH:
            phase3(l)
    if "y" in PH:
        xT_to_y()
    S.final_wait_all("sp")
    S.replay(nc, st)
    st.close()
    print("instructions:", S.n_instr, {e: len(S.ops[e]) for e in S.ENG}, flush=True)
    return nc, hc


_CACHE = {}


def kernel(x_prompt, x_sample, c_prompt, c_sample, w_mod, b_mod, norm_mix, norm_ffn, w_in, w_out,
           qk_gain, sink_a, relpos_table, diff_lambda, diff_subln, w_gate_up, w_down):
    TV, SEG, DEPTH = 8192, 4096, 4
    if "nc" not in _CACHE:
        _CACHE["nc"] = build(TV, SEG, DEPTH)
    nc, _ = _CACHE["nc"]
    f = lambda a: np.ascontiguousarray(np.asarray(a, dtype=np.float32))
    shared = dict(w_mod=f(w_mod), b_mod=f(b_mod), norm_mix=f(norm_mix), norm_ffn=f(norm_ffn), w_in=f(w_in),
                  w_out=f(w_out), qk_gain=f(qk_gain), sink_a=f(sink_a), relpos_table=f(relpos_table),
                  diff_lambda=f(diff_lambda), diff_subln=f(diff_subln), w_gate_up=f(w_gate_up), w_down=f(w_down))
    xp, xs, cp, cs = f(x_prompt), f(x_sample), f(c_prompt), f(c_sample)
    cpair = {("k_" + k): v for k, v in host_consts(TV, SEG, True).items()}
    csamp = {("k_" + k): v for k, v in host_consts(TV, SEG, False).items()}
    in_maps = []
    for core in range(8):
        if core < 2:
            xx = np.concatenate([xp[2 * core], xp[2 * core + 1]], 0)
            cc = np.stack([cp[2 * core], cp[2 * core + 1]], 0)
            kc = cpair
        else:
            xx = xs[0]
            cc = np.stack([cs[0], cs[0]], 0)
            kc = csamp
        m = dict(x=np.ascontiguousarray(xx), c2=np.ascontiguousarray(cc))
        m.update(shared)
        m.update(kc)
        in_maps.append(m)
    res = run_bass_kernel_spmd(nc, in_maps, core_ids=list(range(8)))
    y0 = res.results[0]["y"]
    y1 = res.results[1]["y"]
    y2 = res.results[2]["y"]
    y_prompt = np.stack([y0[:SEG], y0[SEG:], y1[:SEG], y1[SEG:]], 0).astype(np.float32)
    y_sample = y2[None].astype(np.float32)
    return (y_prompt, y_sample)
```
